# Optimizing a Trainium2 kernel written in Bass

```python
import math
import jax, jax.numpy as jnp
from jax import lax
import numpy as np

D_MODEL = 2048
BATCH = 4
SEQ = 8192
DEPTH = 2
DEC_BATCH = 16
DEC_SEQ = 2048
PAST_LEN = 128

GRID_W = 64
HEAD_DIM = 64
MIX_WIDTH = 3072
Q_BLOCK = 128
EPS = 1e-6
ROPE_THETA = 500000.0
ROPE_DIMS = HEAD_DIM // 4
AXIAL_THETA = 10000.0
A_HEADS = 6
A_QK = 2 * HEAD_DIM
A_V = 2 * HEAD_DIM
B_HEADS = 12
B_PATTERNS = ((128, 1), (512, 4), (2048, 16))
C_HEADS = 6
C_DK = 64
C_DV = 128
C_RANK = 16
C_CHUNK = 64
C_GATE_NORM = 16.0
D_Q_HEADS = 12
D_KV_HEADS = 4
D_FF = 5632
IN_SPLITS = (A_HEADS * A_QK, A_HEADS * A_QK, A_HEADS * A_V,
             B_HEADS * HEAD_DIM, B_HEADS * HEAD_DIM, B_HEADS * HEAD_DIM,
             C_HEADS * C_DK, C_HEADS * C_DK, C_HEADS * C_DV, C_HEADS * C_DV, 2 * C_RANK,
             D_Q_HEADS * HEAD_DIM, D_KV_HEADS * HEAD_DIM, D_KV_HEADS * HEAD_DIM)
N_IN = 8224

kernel_name = 'hybrid_parallel_heads_bidir_encoder'


def _rms_norm(x, g):
    xf = x.astype(jnp.float32)
    y = xf * lax.rsqrt(jnp.mean(xf * xf, axis=-1, keepdims=True) + EPS)
    return (y * g.astype(jnp.float32)).astype(x.dtype)


def _swiglu(x, wg, wu, wd):
    return (jax.nn.silu(x @ wg) * (x @ wu)) @ wd


def _heads(t, n):
    b, l, _ = t.shape
    return t.reshape(b, l, n, -1).transpose(0, 2, 1, 3)


def _merge(t):
    b, h, l, d = t.shape
    return t.transpose(0, 2, 1, 3).reshape(b, l, h * d)


def _rope_cos_sin(pos, dims, theta):
    inv = theta ** (-jnp.arange(0, dims, 2, dtype=jnp.float32) / dims)
    ang = pos.astype(jnp.float32)[:, None] * inv[None, :]
    return jnp.cos(ang), jnp.sin(ang)


def _rotate(x, cos, sin):
    half = x.shape[-1] // 2
    x1 = x[..., :half].astype(jnp.float32)
    x2 = x[..., half:].astype(jnp.float32)
    return jnp.concatenate([x1 * cos - x2 * sin, x2 * cos + x1 * sin], axis=-1).astype(x.dtype)


def _partial_rope(x, cos, sin):
    return jnp.concatenate([_rotate(x[..., :ROPE_DIMS], cos, sin), x[..., ROPE_DIMS:]], axis=-1)


def _axial_rope(x, r_cos, r_sin, c_cos, c_sin):
    half = HEAD_DIM // 2
    return jnp.concatenate([_rotate(x[..., :half], r_cos, r_sin),
                            _rotate(x[..., half:], c_cos, c_sin)], axis=-1)


def _diff_attention(q, k, v, lam):
    b, h, L, _ = q.shape
    nb = L // Q_BLOCK
    scale = HEAD_DIM ** -0.5
    qs = jnp.stack([q[..., :HEAD_DIM], q[..., HEAD_DIM:]], 0)
    ks = jnp.stack([k[..., :HEAD_DIM], k[..., HEAD_DIM:]], 0)
    qs = qs.reshape(2, b, h, nb, Q_BLOCK, HEAD_DIM).transpose(3, 0, 1, 2, 4, 5)

    def block(qb):
        s = jnp.einsum('nbhqd,nbhkd->nbhqk', qb, ks).astype(jnp.float32) * scale
        p = jax.nn.softmax(s, axis=-1)
        a = p[0] - lam * p[1]
        return jnp.einsum('bhqk,bhkv->bhqv', a.astype(v.dtype), v)

    o = lax.map(block, qs)
    return o.transpose(1, 2, 0, 3, 4).reshape(b, h, L, v.shape[-1])


def _gqa_attention(q, k, v):
    b, hq, L, d = q.shape
    g = hq // D_KV_HEADS
    nb = L // Q_BLOCK
    scale = d ** -0.5
    qs = q.reshape(b, D_KV_HEADS, g, nb, Q_BLOCK, d).transpose(3, 0, 1, 2, 4, 5)

    def block(qb):
        s = jnp.einsum('bkgqd,bksd->bkgqs', qb, k).astype(jnp.float32) * scale
        p = jax.nn.softmax(s, axis=-1)
        return jnp.einsum('bkgqs,bksd->bkgqd', p.astype(v.dtype), v)

    o = lax.map(block, qs)
    return o.transpose(1, 2, 3, 0, 4, 5).reshape(b, hq, L, d)


def _banded_window_attention(q, k, v, n_side):
    lead = q.shape[:-2]
    N, d = q.shape[-2], q.shape[-1]
    blk = n_side
    nb = -(-N // blk)
    P = nb * blk
    nlead = len(lead)
    qp = jnp.pad(q, [(0, 0)] * nlead + [(0, P - N), (0, 0)]).reshape(*lead, nb, blk, d)
    kv_pad = [(0, 0)] * nlead + [(blk, P - N + blk), (0, 0)]
    kp = jnp.pad(k, kv_pad).reshape(*lead, nb + 2, blk, d)
    vp = jnp.pad(v, kv_pad).reshape(*lead, nb + 2, blk, d)
    kw = jnp.concatenate([kp[..., :-2, :, :], kp[..., 1:-1, :, :], kp[..., 2:, :, :]], axis=-2)
    vw = jnp.concatenate([vp[..., :-2, :, :], vp[..., 1:-1, :, :], vp[..., 2:, :, :]], axis=-2)
    s = jnp.einsum('...nqd,...nkd->...nqk', qp, kw).astype(jnp.float32) * (d ** -0.5)
    qpos = jnp.arange(nb)[:, None] * blk + jnp.arange(blk)[None, :]
    kpos = jnp.arange(nb)[:, None] * blk - blk + jnp.arange(3 * blk)[None, :]
    rel = kpos[:, None, :] - qpos[:, :, None]
    valid = (jnp.abs(rel) <= n_side) & (kpos[:, None, :] >= 0) & (kpos[:, None, :] < N)
    s = jnp.where(valid, s, -1e30)
    m = jnp.max(s, axis=-1, keepdims=True)
    e = jnp.exp(s - m)
    den = jnp.sum(e, axis=-1, keepdims=True)
    p = e / den
    lse = (m + jnp.log(den))[..., 0]
    o = jnp.einsum('...nqk,...nkd->...nqd', p.astype(v.dtype), vw)
    o = o.reshape(*lead, P, d)[..., :N, :]
    lse = lse.reshape(*lead, P)[..., :N]
    return o, lse


def _dilated_attention(q, k, v):
    b, h, L, d = q.shape
    outs, lses = [], []
    for window, dil in B_PATTERNS:
        n_side = window // (2 * dil)

        def to_sub(t):
            return t.reshape(b, h, L // dil, dil, d).swapaxes(2, 3)

        o, lse = _banded_window_attention(to_sub(q), to_sub(k), to_sub(v), n_side)
        outs.append(o.swapaxes(2, 3).reshape(b, h, L, d))
        lses.append(lse.swapaxes(2, 3).reshape(b, h, L))
    w = jax.nn.softmax(jnp.stack(lses, 0), axis=0)
    return jnp.einsum('gbhl,gbhld->bhld', w.astype(q.dtype), jnp.stack(outs, 0))


def _gla_scan(q, k, v, g):
    b, h, L, dk = q.shape
    dv = v.shape[-1]
    n = L // C_CHUNK
    q = q.reshape(b, h, n, C_CHUNK, dk)
    k = k.reshape(b, h, n, C_CHUNK, dk)
    v = v.reshape(b, h, n, C_CHUNK, dv)
    g = g.reshape(b, h, n, C_CHUNK, dk)
    cum = jnp.cumsum(g, axis=-2)
    last = cum[..., -1:, :]
    q_dec = q * jnp.exp(cum)
    att = jnp.einsum('bhncd,bhnsd->bhncs', q_dec, k * jnp.exp(-cum))
    att = jnp.where(jnp.tril(jnp.ones((C_CHUNK, C_CHUNK), dtype=bool)), att, 0.0)
    o_intra = jnp.einsum('bhncs,bhnse->bhnce', att, v)
    u = jnp.einsum('bhncd,bhnce->bhnde', k * jnp.exp(last - cum), v)
    decay = jnp.exp(last[..., 0, :])

    def step(state, inp):
        dec, uu = inp
        return dec[..., None] * state + uu, state

    _, s_in = lax.scan(step, jnp.zeros((b, h, dk, dv), jnp.float32),
                       (jnp.moveaxis(decay, 2, 0), jnp.moveaxis(u, 2, 0)))
    s_in = jnp.moveaxis(s_in, 0, 2)
    o_inter = jnp.einsum('bhncd,bhnde->bhnce', q_dec, s_in)
    return (o_intra + o_inter).reshape(b, h, L, dv)


def _bidir_gla(q, k, v, g_f, g_b):
    dt = v.dtype
    q = q.astype(jnp.float32) * (C_DK ** -0.5)
    k = k.astype(jnp.float32)
    v = v.astype(jnp.float32)
    fwd = _gla_scan(q, k, v, g_f)

    def flip(t):
        return jnp.flip(t, axis=2)

    bwd = flip(_gla_scan(flip(q), flip(k), flip(v), flip(g_b)))
    return (fwd + bwd).astype(dt)


def _trunk(x, ffn1_norm, ffn1_w_gate, ffn1_w_up, ffn1_w_down, mix_norm, w_in, w_out,
           diff_lambda_q1, diff_lambda_k1, diff_lambda_q2, diff_lambda_k2, diff_out_norm,
           gla_gate_up_f, gla_gate_bias_f, gla_gate_up_b, gla_gate_bias_b, gla_out_norm,
           gqa_q_norm, gqa_k_norm, ffn2_norm, ffn2_w_gate, ffn2_w_up, ffn2_w_down, final_norm):
    L = x.shape[1]
    rows = L // GRID_W
    t = jnp.arange(L, dtype=jnp.float32)
    p_cos, p_sin = _rope_cos_sin(t, ROPE_DIMS, ROPE_THETA)
    row_pos = jnp.repeat(jnp.arange(rows, dtype=jnp.float32), GRID_W)
    col_pos = jnp.tile(jnp.arange(GRID_W, dtype=jnp.float32), rows)
    r_cos, r_sin = _rope_cos_sin(row_pos, HEAD_DIM // 2, AXIAL_THETA)
    c_cos, c_sin = _rope_cos_sin(col_pos, HEAD_DIM // 2, AXIAL_THETA)
    split_points = np.cumsum(np.array(IN_SPLITS))[:-1].tolist()

    for l in range(DEPTH):
        x = x + 0.5 * _swiglu(_rms_norm(x, ffn1_norm[l]), ffn1_w_gate[l], ffn1_w_up[l], ffn1_w_down[l])

        h = _rms_norm(x, mix_norm[l])
        u = h @ w_in[l]
        (a_q, a_k, a_v, b_q, b_k, b_v, c_q, c_k, c_v, c_g, c_low,
         d_q, d_k, d_v) = jnp.split(u, split_points, axis=-1)

        aq = _heads(a_q, A_HEADS)
        ak = _heads(a_k, A_HEADS)
        av = _heads(a_v, A_HEADS)
        aq = jnp.concatenate([_partial_rope(aq[..., :HEAD_DIM], p_cos, p_sin),
                              _partial_rope(aq[..., HEAD_DIM:], p_cos, p_sin)], axis=-1)
        ak = jnp.concatenate([_partial_rope(ak[..., :HEAD_DIM], p_cos, p_sin),
                              _partial_rope(ak[..., HEAD_DIM:], p_cos, p_sin)], axis=-1)
        lam_init = 0.8 - 0.6 * math.exp(-0.3 * l)
        lam = (jnp.exp(jnp.sum(diff_lambda_q1[l].astype(jnp.float32) * diff_lambda_k1[l].astype(jnp.float32)))
               - jnp.exp(jnp.sum(diff_lambda_q2[l].astype(jnp.float32) * diff_lambda_k2[l].astype(jnp.float32)))
               + lam_init)
        o_a = _diff_attention(aq, ak, av, lam)
        o_a = _merge(_rms_norm(o_a, diff_out_norm[l]) * (1.0 - lam_init))

        bq = _partial_rope(_heads(b_q, B_HEADS), p_cos, p_sin)
        bk = _partial_rope(_heads(b_k, B_HEADS), p_cos, p_sin)
        bv = _heads(b_v, B_HEADS)
        o_b = _merge(_dilated_attention(bq, bk, bv))

        low_f = c_low[..., :C_RANK]
        low_b = c_low[..., C_RANK:]
        g_f = _heads(jax.nn.log_sigmoid((low_f @ gla_gate_up_f[l] + gla_gate_bias_f[l]).astype(jnp.float32)) / C_GATE_NORM, C_HEADS)
        g_b = _heads(jax.nn.log_sigmoid((low_b @ gla_gate_up_b[l] + gla_gate_bias_b[l]).astype(jnp.float32)) / C_GATE_NORM, C_HEADS)
        o_c = _bidir_gla(_heads(c_q, C_HEADS), _heads(c_k, C_HEADS), _heads(c_v, C_HEADS), g_f, g_b)
        o_c = _merge(_rms_norm(o_c, gla_out_norm[l]) * jax.nn.silu(_heads(c_g, C_HEADS)))

        dq = _axial_rope(_rms_norm(_heads(d_q, D_Q_HEADS), gqa_q_norm[l]), r_cos, r_sin, c_cos, c_sin)
        dk = _axial_rope(_rms_norm(_heads(d_k, D_KV_HEADS), gqa_k_norm[l]), r_cos, r_sin, c_cos, c_sin)
        dv = _heads(d_v, D_KV_HEADS)
        o_d = _merge(_gqa_attention(dq, dk, dv))

        x = x + jnp.concatenate([o_a, o_b, o_c, o_d], axis=-1) @ w_out[l]

        x = x + 0.5 * _swiglu(_rms_norm(x, ffn2_norm[l]), ffn2_w_gate[l], ffn2_w_up[l], ffn2_w_down[l])
    return _rms_norm(x, final_norm)


def setup_inputs(seed: int = 0) -> dict:
    key = jax.random.key(seed)
    ks = jax.random.split(key, 26)

    def nrm(k, shape, scale):
        return jax.random.normal(k, shape, jnp.float32) * scale

    def gain(k, shape):
        return 1.0 + 0.02 * jax.random.normal(k, shape, jnp.float32)

    return {
        'x_prompt': nrm(ks[0], (BATCH, SEQ, D_MODEL), 1.0),
        'x_sample': nrm(ks[1], (DEC_BATCH, DEC_SEQ, D_MODEL), 1.0),
        'ffn1_norm': gain(ks[2], (DEPTH, D_MODEL)),
        'ffn1_w_gate': nrm(ks[3], (DEPTH, D_MODEL, D_FF), D_MODEL ** -0.5),
        'ffn1_w_up': nrm(ks[4], (DEPTH, D_MODEL, D_FF), D_MODEL ** -0.5),
        'ffn1_w_down': nrm(ks[5], (DEPTH, D_FF, D_MODEL), D_FF ** -0.5),
        'mix_norm': gain(ks[6], (DEPTH, D_MODEL)),
        'w_in': nrm(ks[7], (DEPTH, D_MODEL, N_IN), D_MODEL ** -0.5),
        'w_out': nrm(ks[8], (DEPTH, MIX_WIDTH, D_MODEL), MIX_WIDTH ** -0.5),
        'diff_lambda_q1': nrm(ks[9], (DEPTH, HEAD_DIM), 0.1),
        'diff_lambda_k1': nrm(ks[10], (DEPTH, HEAD_DIM), 0.1),
        'diff_lambda_q2': nrm(ks[11], (DEPTH, HEAD_DIM), 0.1),
        'diff_lambda_k2': nrm(ks[12], (DEPTH, HEAD_DIM), 0.1),
        'diff_out_norm': gain(ks[13], (DEPTH, A_V)),
        'gla_gate_up_f': nrm(ks[14], (DEPTH, C_RANK, C_HEADS * C_DK), C_RANK ** -0.5),
        'gla_gate_bias_f': nrm(ks[15], (DEPTH, C_HEADS * C_DK), 0.1),
        'gla_gate_up_b': nrm(ks[16], (DEPTH, C_RANK, C_HEADS * C_DK), C_RANK ** -0.5),
        'gla_gate_bias_b': nrm(ks[17], (DEPTH, C_HEADS * C_DK), 0.1),
        'gla_out_norm': gain(ks[18], (DEPTH, C_DV)),
        'gqa_q_norm': gain(ks[19], (DEPTH, HEAD_DIM)),
        'gqa_k_norm': gain(ks[20], (DEPTH, HEAD_DIM)),
        'ffn2_norm': gain(ks[21], (DEPTH, D_MODEL)),
        'ffn2_w_gate': nrm(ks[22], (DEPTH, D_MODEL, D_FF), D_MODEL ** -0.5),
        'ffn2_w_up': nrm(ks[23], (DEPTH, D_MODEL, D_FF), D_MODEL ** -0.5),
        'ffn2_w_down': nrm(ks[24], (DEPTH, D_FF, D_MODEL), D_FF ** -0.5),
        'final_norm': gain(ks[25], (D_MODEL,)),
    }


def reference(x_prompt, x_sample, ffn1_norm, ffn1_w_gate, ffn1_w_up, ffn1_w_down, mix_norm, w_in, w_out,
              diff_lambda_q1, diff_lambda_k1, diff_lambda_q2, diff_lambda_k2, diff_out_norm,
              gla_gate_up_f, gla_gate_bias_f, gla_gate_up_b, gla_gate_bias_b, gla_out_norm,
              gqa_q_norm, gqa_k_norm, ffn2_norm, ffn2_w_gate, ffn2_w_up, ffn2_w_down, final_norm):
    y_prompt = _trunk(x_prompt, ffn1_norm, ffn1_w_gate, ffn1_w_up, ffn1_w_down, mix_norm, w_in, w_out,
                      diff_lambda_q1, diff_lambda_k1, diff_lambda_q2, diff_lambda_k2, diff_out_norm,
                      gla_gate_up_f, gla_gate_bias_f, gla_gate_up_b, gla_gate_bias_b, gla_out_norm,
                      gqa_q_norm, gqa_k_norm, ffn2_norm, ffn2_w_gate, ffn2_w_up, ffn2_w_down, final_norm)
    y_sample = _trunk(x_sample, ffn1_norm, ffn1_w_gate, ffn1_w_up, ffn1_w_down, mix_norm, w_in, w_out,
                      diff_lambda_q1, diff_lambda_k1, diff_lambda_q2, diff_lambda_k2, diff_out_norm,
                      gla_gate_up_f, gla_gate_bias_f, gla_gate_up_b, gla_gate_bias_b, gla_out_norm,
                      gqa_q_norm, gqa_k_norm, ffn2_norm, ffn2_w_gate, ffn2_w_up, ffn2_w_down, final_norm)
    return (y_prompt, y_sample)
```

```python
import contextlib
import math
import numpy as np
import concourse.bass as bass
import concourse.mybir as mybir
from concourse.bass_utils import run_bass_kernel_spmd

F32 = mybir.dt.float32
BF16 = mybir.dt.bfloat16
AF = mybir.ActivationFunctionType
ALU = mybir.AluOpType

D = 2048
KD = 16
DFF = 5632
KF = 44
DEPTH = 2
EPS = 1e-6
T = 512
ENGS = ("pe", "act", "dve", "pool", "sp")
C_ID, C_ONES, C_BLK, C_RP, C_RA, C_TRIU, C_TRIL, C_SL, C_SU = range(9)
NFM = 44 * 128 + 32
NTM = 2944
import os
SKIP_SAME = os.environ.get("SKIP_SAME", "0") == "1"
LAM_INIT = [0.8 - 0.6 * math.exp(-0.3 * l) for l in range(DEPTH)]


class Op:
    __slots__ = ("eng", "fn", "deps", "marked", "seq", "dma_key", "dma_n", "group", "g")

    def __init__(self, eng, fn, dma_key, group, g):
        self.eng = eng
        self.fn = fn
        self.deps = []
        self.marked = False
        self.seq = 0
        self.dma_key = dma_key
        self.dma_n = 0
        self.group = group
        self.g = g


class Prog:
    def __init__(self, nc):
        self.nc = nc
        self.ops = {e: [] for e in ENGS}
        self.last_w = {}
        self.readers = {}
        self.dma_count = {}
        self.last_dma = {}
        self.g = 0
        self.bar_id = 0
        self.bar_deps = []
        self.eng_bar = {e: 0 for e in ENGS}

    @staticmethod
    def _src(o):
        return ("dma", o.dma_key) if o.dma_key is not None else o.eng

    def barrier(self):
        deps = []
        for e in ENGS:
            if self.ops[e]:
                deps.append(self.ops[e][-1])
        deps.extend(self.last_dma.values())
        self.bar_deps = deps
        self.bar_id += 1

    def op(self, eng, fn, reads=(), writes=(), dma_key=None, group=False, extra=()):
        self.g += 1
        o = Op(eng, fn, dma_key, group, self.g)
        cand = list(extra)
        for k in reads:
            w = self.last_w.get(k)
            if w is not None:
                cand.append(w)
        for k in writes:
            w = self.last_w.get(k)
            if w is not None:
                cand.append(w)
            r = self.readers.get(k)
            if r:
                cand.extend(r.values())
        if self.eng_bar[eng] != self.bar_id:
            self.eng_bar[eng] = self.bar_id
            cand.extend(self.bar_deps)
        best = {}
        for d in cand:
            if d is o:
                continue
            if d.dma_key is None and d.eng == eng and (eng == "pe" or SKIP_SAME):
                continue
            s = self._src(d)
            b = best.get(s)
            if b is None or d.g > b.g:
                best[s] = d
        for d in best.values():
            o.deps.append(d)
            if d.dma_key is None:
                d.marked = True
        for k in writes:
            self.last_w[k] = o
            self.readers[k] = {}
        src = self._src(o)
        for k in reads:
            self.readers.setdefault(k, {})[src] = o
        if dma_key is not None:
            n = self.dma_count.get(dma_key, 0) + 1
            self.dma_count[dma_key] = n
            o.dma_n = n
            self.last_dma[dma_key] = o
        self.ops[eng].append(o)
        return o

    def run(self):
        nc = self.nc
        finals = list(self.last_dma.values())
        for e in ENGS:
            if self.ops[e]:
                d = self.ops[e][-1]
                if d.dma_key is None:
                    d.marked = True
                    finals.append(d)
        for e in ENGS:
            c = 0
            for o in self.ops[e]:
                if o.marked:
                    c += 1
                    o.seq = c
        names = list(ENGS) + ["dma:" + k for k in self.dma_count]
        with contextlib.ExitStack() as st:
            sems = {}
            for i, n in enumerate(names):
                sems[n] = st.enter_context(nc.semaphore("s%d" % i))
            block = st.enter_context(nc.Block())

            def dep_wait(d):
                if d.dma_key is not None:
                    n = self.dma_count[d.dma_key] if d.group else d.dma_n
                    return "dma:" + d.dma_key, 16 * n
                return d.eng, d.seq

            def make(e):
                def body(engine):
                    waited = {}
                    for o in self.ops[e]:
                        for d in o.deps:
                            sn, val = dep_wait(d)
                            if waited.get(sn, 0) >= val:
                                continue
                            waited[sn] = val
                            engine.wait_ge(sems[sn], val)
                        ins = o.fn(engine)
                        if o.dma_key is not None:
                            ins.then_inc(sems["dma:" + o.dma_key], 16)
                        elif o.marked:
                            ins.then_inc(sems[e], 1)
                    if e == "sp":
                        for d in finals:
                            sn, val = dep_wait(d)
                            if waited.get(sn, 0) >= val:
                                continue
                            waited[sn] = val
                            engine.wait_ge(sems[sn], val)
                return body

            block.tensor(make("pe"))
            block.scalar(make("act"))
            block.vector(make("dve"))
            block.gpsimd(make("pool"))
            block.sync(make("sp"))


def build(NTOK, debug=False, stages=("conv", "p1", "attn", "gla", "p3"), nlayers=DEPTH):
    NT = NTOK // T
    NK = NTOK // 128
    nc = bass.Bass("TRN2", target_bir_lowering=False)
    P = Prog(nc)

    def din(name, shape, dt=F32):
        return nc.dram_tensor(name, list(shape), dt, kind="ExternalInput").ap()

    def dscr(name, shape, dt):
        if debug and not name.startswith("s_"):
            return nc.dram_tensor(name, list(shape), dt, kind="ExternalOutput").ap()
        return nc.dram_tensor(name, list(shape), dt).ap()

    x_in = din("x", [NTOK, D])
    y_out = nc.dram_tensor("y", [NTOK, D], F32, kind="ExternalOutput").ap()
    if "conv" in stages:
        wg = [din("wg1", [DEPTH, D, DFF]), din("wg2", [DEPTH, D, DFF])]
        wu = [din("wu1", [DEPTH, D, DFF]), din("wu2", [DEPTH, D, DFF])]
        wd = [din("wd1", [DEPTH, DFF, D]), din("wd2", [DEPTH, DFF, D])]
        winfm = din("winfm", [DEPTH, D, NFM])
        wintm = din("wintm", [DEPTH, D, NTM])
        wout = din("wout", [DEPTH, 3072, D])
    NSP = DEPTH * (3 * 16 + 4) + 16
    smallp = din("smallp", [128, NSP])
    cmat = din("cmat", [128, 9, 128])
    lamp = din("lamp", [64, 4 * DEPTH])
    upaug = din("upaug", [33, DEPTH, 2, 384])
    ropeP = din("ropeP", [2, 128, NTOK])
    ropeD = din("ropeD", [2, 128, NTOK])
    segbias = din("segbias", [128, NK * NT])
    bmask = din("bmask", [128, 20, 512])
    gflag = din("gflag", [128, 2, NK])

    s_gu = [[dscr("s_gu%d%d" % (l, f), [KF, 128, 16, 256], BF16) for f in range(2)] for l in range(DEPTH)]
    s_d = [[dscr("s_d%d%d" % (l, f), [16, 128, KF, 128], BF16) for f in range(2)] for l in range(DEPTH)]
    s_fm = [dscr("s_fm%d" % l, [23, 128, 16, 256], BF16) for l in range(DEPTH)]
    s_tm = [dscr("s_tm%d" % l, [12, 128, 16, 256], BF16) for l in range(DEPTH)]
    s_o = [dscr("s_o%d" % l, [16, 128, 24, 128], BF16) for l in range(DEPTH)]
    xs = dscr("xs", [128, 16, NTOK], F32)
    aq = dscr("aq", [128, 6, NTOK], BF16)
    ak = dscr("ak", [128, 6, NTOK], BF16)
    bq = dscr("bq", [128, 6, NTOK], BF16)
    bk = dscr("bk", [128, 6, NTOK], BF16)
    dq = dscr("dq", [128, 6, NTOK], BF16)
    dk = dscr("dk", [128, 2, NTOK], BF16)
    cqs = [dscr("cq%d" % d, [128, 3, NTOK], BF16) for d in range(2)]
    cks = [dscr("ck%d" % d, [128, 3, NTOK], BF16) for d in range(2)]
    sgs = dscr("sgs", [128, 6, NTOK], BF16)
    av = dscr("av", [6, 128, NK, 128], BF16)
    bv = dscr("bv", [6, 128, NK, 128], BF16)
    cv = dscr("cv", [6, 128, NK, 128], BF16)
    dvs = dscr("dvs", [2, 128, NK, 128], BF16)
    khs = [dscr("kh%d" % d, [3, 128, NK, 128], BF16) for d in range(2)]
    oT = dscr("oT", [128, 24, NTOK], BF16)
    ofs = dscr("ofs", [128, 6, NTOK], F32)

    def sb(name, shape, dt):
        return nc.alloc_sbuf_tensor(name, list(shape), dt)

    ps = nc.alloc_psum_tensor("ps", [128, 8, 512], F32)
    cm = sb("cm", [128, 9, 128], BF16)
    identf = sb("identf", [128, 128], F32)
    onesf = sb("onesf", [64, 128], F32)
    smp = sb("smp", [128, NSP], F32)
    segb = sb("segb", [128, NK * NT], F32)
    gfl = sb("gfl", [128, 2, NK], F32)
    dec = sb("dec", [128, 2, 3, NK], F32)
    neglam = sb("neglam", [128, DEPTH], F32)
    lam_sb = sb("lam_sb", [64, 4 * DEPTH], F32)
    lamt = sb("lamt", [128, 8], F32)
    upa = sb("upa", [33, DEPTH, 2, 384], F32)
    epsb = sb("epsb", [128, 1 + DEPTH], F32)

    AW = 46080
    arena = sb("arena", [128, AW], F32)

    class Carver:
        def __init__(self, base=0):
            self.off = base

        def f32(self, *shape):
            n = int(np.prod(shape))
            v = arena[:, self.off:self.off + n]
            self.off += n
            assert self.off <= AW, self.off
            if len(shape) == 2:
                return v.rearrange("p (a b) -> p a b", b=shape[1])
            if len(shape) == 3:
                return v.rearrange("p (a b c) -> p a b c", b=shape[1], c=shape[2])
            return v

        def bf(self, *shape):
            n = int(np.prod(shape))
            assert n % 2 == 0
            v = arena[:, self.off:self.off + n // 2].bitcast(BF16)
            self.off += n // 2
            assert self.off <= AW, self.off
            if len(shape) == 2:
                return v.rearrange("p (a b) -> p a b", b=shape[1])
            if len(shape) == 3:
                return v.rearrange("p (a b c) -> p a b c", b=shape[1], c=shape[2])
            return v

    def dma(eng, out, in_, reads, writes, key, group=False, extra=()):
        return P.op(eng, lambda e: e.dma_start(out=out, in_=in_), reads, writes, dma_key=key, group=group, extra=extra)

    def mm(out, lhsT, rhs, start, stop, reads, writes):
        return P.op("pe", lambda e: e.matmul(out, lhsT=lhsT, rhs=rhs, start=start, stop=stop), reads, writes)

    def act(out, in_, func, reads, writes, scale=1.0, bias=None):
        if bias is None:
            return P.op("act", lambda e: e.activation(out=out, in_=in_, func=func, scale=scale), reads, writes)
        return P.op("act", lambda e: e.activation(out=out, in_=in_, func=func, scale=scale, bias=bias), reads, writes)

    def tt(eng, out, in0, in1, op, reads, writes):
        return P.op(eng, lambda e: e.tensor_tensor(out=out, in0=in0, in1=in1, op=op), reads, writes)

    def stt(out, in0, scalar, in1, op0, op1, reads, writes):
        return P.op("dve", lambda e: e.scalar_tensor_tensor(out=out, in0=in0, scalar=scalar, in1=in1, op0=op0, op1=op1), reads, writes)

    def ts(out, in0, scalar1, op0, reads, writes):
        return P.op("dve", lambda e: e.tensor_scalar(out=out, in0=in0, scalar1=scalar1, scalar2=None, op0=op0), reads, writes)

    def recip(out, in_, reads, writes):
        return P.op("dve", lambda e: e.reciprocal(out=out, in_=in_), reads, writes)

    def cp(eng, out, in_, reads, writes):
        if eng == "act":
            return P.op("act", lambda e: e.copy(out=out, in_=in_), reads, writes)
        return P.op(eng, lambda e: e.tensor_copy(out=out, in_=in_), reads, writes)

    bank_i = [0]

    def bank():
        b = bank_i[0] % 8
        bank_i[0] += 1
        return b

    def pk(b):
        return ("ps", b)

    class Ring:
        def __init__(self, name, n):
            self.name, self.n, self.i = name, n, 0

        def next(self):
            s = self.i % self.n
            self.i += 1
            return s

    dma("pool", cm[:], cmat[:, :, :], [], ["cm"], "c_cm")
    dma("sp", identf[:], cmat[:, C_ID, :], [], ["identf"], "c_id")
    dma("sp", onesf[:], cmat[0:64, C_ONES, :], [], ["onesf"], "c_on")
    dma("sp", smp[:], smallp[:, :], [], ["smp"], "c_sm")
    dma("sp", segb[:], segbias[:, :], [], ["segb"], "c_sg")
    dma("sp", gfl[:], gflag[:, :, :], [], ["gfl"], "c_gf")
    dma("sp", lam_sb[:], lamp[:, :], [], ["lam_sb"], "c_la")
    dma("sp", upa[:], upaug[:, :, :, :], [], ["upa"], "c_up")
    P.op("dve", lambda e: e.memset(epsb[:, 0:1], EPS), [], ["epsb"])
    for l in range(DEPTH):
        c = 1.0 - LAM_INIT[l]
        P.op("dve", (lambda l, c: lambda e: e.memset(epsb[:, 1 + l:2 + l], EPS / (c * c)))(l, c), [], ["epsb"])
    if "p1" not in stages:
        P.op("dve", lambda e: e.memset(dec[:, :, :, :], 0.9), [], ["dec"])
    for l in range(DEPTH):
        tt("dve", lamt[0:64, 0:1], lam_sb[:, 4 * l:4 * l + 1], lam_sb[:, 4 * l + 1:4 * l + 2], ALU.mult, ["lam_sb"], ["lamt"])
        tt("dve", lamt[0:64, 1:2], lam_sb[:, 4 * l + 2:4 * l + 3], lam_sb[:, 4 * l + 3:4 * l + 4], ALU.mult, ["lam_sb", "lamt"], ["lamt"])
        b = bank()
        mm(ps[:, b, 0:2], onesf[:], lamt[0:64, 0:2], True, True, ["onesf", "lamt"], [pk(b)])
        act(lamt[:, 2:4], ps[:, b, 0:2], AF.Exp, [pk(b)], ["lamt"])
        tt("dve", lamt[:, 4:5], lamt[:, 3:4], lamt[:, 2:3], ALU.subtract, ["lamt"], ["lamt"])
        P.op("dve", (lambda l: lambda e: e.tensor_scalar(out=neglam[:, l:l + 1], in0=lamt[:, 4:5], scalar1=-LAM_INIT[l], scalar2=None, op0=ALU.add))(l),
             ["lamt"], ["neglam"])

    conv = {}

    def cv_dma(gname, out, in_):
        conv[gname] = dma("pool", out, in_, [], [], "cv_" + gname, group=True)

    def conv_ffn(l, f):
        g = "f%d%d" % (l, f)
        wgv = wg[f][l].rearrange("(kc p) c -> p kc c", p=128)
        wuv = wu[f][l].rearrange("(kc p) c -> p kc c", p=128)
        wdv = wd[f][l].rearrange("(kc p) c -> p kc c", p=128)
        for j in range(KF):
            cv_dma(g, s_gu[l][f][j][:, :, 0:128], wgv[:, :, j * 128:(j + 1) * 128])
            cv_dma(g, s_gu[l][f][j][:, :, 128:256], wuv[:, :, j * 128:(j + 1) * 128])
        for m in range(16):
            cv_dma(g, s_d[l][f][m], wdv[:, :, m * 128:(m + 1) * 128])

    def conv_in(l):
        g = "i%d" % l
        fv = winfm[l].rearrange("(kc p) c -> p kc c", p=128)
        tv = wintm[l].rearrange("(kc p) c -> p kc c", p=128)
        cv_dma(g, s_fm[l][0][:, :, 0:32], fv[:, :, 0:32])
        for u in range(1, 23):
            cv_dma(g, s_fm[l][u], fv[:, :, 32 + (u - 1) * 256:32 + u * 256])
        for u in range(12):
            ncol = 256 if u < 11 else 128
            cv_dma(g, s_tm[l][u][:, :, 0:ncol], tv[:, :, u * 256:u * 256 + ncol])

    def conv_out(l):
        g = "o%d" % l
        ov = wout[l].rearrange("(kc p) c -> p kc c", p=128)
        for m in range(16):
            cv_dma(g, s_o[l][m], ov[:, :, m * 128:(m + 1) * 128])

    if "conv" in stages:
        for l in range(DEPTH):
            conv_ffn(l, 0)
            conv_in(l)
            conv_out(l)
            conv_ffn(l, 1)

    class TL:
        pass

    def carve_tl():
        c = Carver()
        t = TL()
        t.xT = c.f32(16, T)
        t.hT = c.bf(16, T)
        t.hid = c.bf(KF, T)
        t.rA = [c.bf(16, 256) for _ in range(3)]
        t.rB = [c.bf(KF, 128) for _ in range(2)]
        t.rstd = c.f32(T)
        t.sq = [c.bf(T) for _ in range(2)]
        t.sg = [c.bf(T) for _ in range(2)]
        t.ub = [c.bf(T) for _ in range(2)]
        t.stg = [c.bf(T) for _ in range(4)]
        t.tmp = [c.f32(T) for _ in range(3)]
        hoff = [0]
        hid_flat = t.hid

        def hcarve(nbf, shape):
            nch = (nbf + T - 1) // T
            a = hoff[0]
            hoff[0] += nch
            assert hoff[0] <= KF
            v = hid_flat[:, a:a + nch, :]
            return v, [("hid", k) for k in range(a, a + nch)]

        t.hcarve = hcarve
        t.hreset = lambda: hoff.__setitem__(0, 0)
        return t

    tl = carve_tl()
    ringA = Ring("A", 3)
    ringB = Ring("B", 2)
    sqr, sgr, ubr, stgr, tmpr = Ring("sq", 2), Ring("sg", 2), Ring("ub", 2), Ring("stg", 4), Ring("tmp", 3)

    def load_A(src, ncols, gname):
        s = ringA.next()
        dma("sp", tl.rA[s][:, :, 0:ncols], src, [], [("A", s)], "A%d" % s, extra=[conv[gname]])
        return s

    def load_B(src, nk, gname):
        s = ringB.next()
        dma("sp", tl.rB[s][:, 0:nk, :], src, [], [("B", s)], "B%d" % s, extra=[conv[gname]])
        return s

    XK = [("x", k) for k in range(16)]

    def rmsnorm(gcol):
        b = bank()
        for kc in range(16):
            s = sqr.next()
            act(tl.sq[s], tl.xT[:, kc, :], AF.Square, [("x", kc)], [("sq", s)])
            mm(ps[:, b, :], cm[:, C_ONES, :], tl.sq[s], kc == 0, kc == 15, [("sq", s), "cm"], [pk(b)])
        act(tl.rstd, ps[:, b, :], AF.Sqrt, [pk(b), "epsb"], ["rstd"], scale=1.0 / D, bias=epsb[:, 0:1])
        recip(tl.rstd, tl.rstd, ["rstd"], ["rstd"])
        for kc in range(16):
            stt(tl.hT[:, kc, :], tl.xT[:, kc, :], smp[:, gcol + kc:gcol + kc + 1], tl.rstd, ALU.mult, ALU.mult,
                [("x", kc), "rstd", "smp"], ["hT"])

    def ffn(l, f, gcol):
        g = "f%d%d" % (l, f)
        rmsnorm(gcol)
        for j in range(KF):
            s = load_A(s_gu[l][f][j], 256, g)
            bg, bu = bank(), bank()
            for kc in range(16):
                mm(ps[:, bg, :], tl.rA[s][:, kc, 0:128], tl.hT[:, kc, :], kc == 0, kc == 15, [("A", s), "hT"], [pk(bg)])
            for kc in range(16):
                mm(ps[:, bu, :], tl.rA[s][:, kc, 128:256], tl.hT[:, kc, :], kc == 0, kc == 15, [("A", s), "hT"], [pk(bu)])
            q = sgr.next()
            act(tl.sg[q], ps[:, bg, :], AF.Silu, [pk(bg)], [("sg", q)])
            tt("dve", tl.hid[:, j, :], tl.sg[q], ps[:, bu, :], ALU.mult, [("sg", q), pk(bu)], [("hid", j)])
        for m in range(16):
            s = load_B(s_d[l][f][m], KF, g)
            b = bank()
            for kc in range(KF):
                mm(ps[:, b, :], tl.rB[s][:, kc, :], tl.hid[:, kc, :], kc == 0, kc == KF - 1, [("B", s), ("hid", kc)], [pk(b)])
            stt(tl.xT[:, m, :], ps[:, b, :], 0.5, tl.xT[:, m, :], ALU.mult, ALU.add, [pk(b), ("x", m)], [("x", m)])

    def store_stg(slot, dst, dkey=None):
        dma("act", dst, tl.stg[slot], [("stg", slot)], [] if dkey is None else [dkey], "st%d" % slot)

    def load_x_input(tile):
        tl.hreset()
        stgs = [tl.hcarve(2 * D, None) for _ in range(2)]
        for blk in range(4):
            v, keys = stgs[blk % 2]
            xv = v.bitcast(F32).rearrange("p a b -> p (a b)")
            r0 = tile * T + blk * 128
            dma("sp", xv, x_in[r0:r0 + 128, :], [], keys, "xi%d" % (blk % 2))
            for k4 in range(4):
                b = bank()
                for k in range(4):
                    kc = k4 * 4 + k
                    P.op("pe", (lambda b, k, kc, xv: lambda e: e.transpose(ps[:, b, k * 128:(k + 1) * 128], xv[:, kc * 128:(kc + 1) * 128], identf[:]))(b, k, kc, xv),
                         keys + ["identf"], [pk(b)])
                cp("dve", tl.xT[:, k4 * 4:k4 * 4 + 4, blk * 128:(blk + 1) * 128],
                   ps[:, b, :].rearrange("p (a b) -> p a b", b=128), [pk(b)], [("x", k4 * 4 + k) for k in range(4)])

    def final_out(tile):
        gcol = DEPTH * 52
        b = bank()
        for kc in range(16):
            s = sqr.next()
            act(tl.sq[s], tl.xT[:, kc, :], AF.Square, [("x", kc)], [("sq", s)])
            mm(ps[:, b, :], cm[:, C_ONES, :], tl.sq[s], kc == 0, kc == 15, [("sq", s), "cm"], [pk(b)])
        act(tl.rstd, ps[:, b, :], AF.Sqrt, [pk(b), "epsb"], ["rstd"], scale=1.0 / D, bias=epsb[:, 0:1])
        recip(tl.rstd, tl.rstd, ["rstd"], ["rstd"])
        for kc in range(16):
            stt(tl.xT[:, kc, :], tl.xT[:, kc, :], smp[:, gcol + kc:gcol + kc + 1], tl.rstd, ALU.mult, ALU.mult,
                [("x", kc), "rstd", "smp"], [("x", kc)])
        tl.hreset()
        stgs = [tl.hcarve(2 * D, None) for _ in range(2)]
        for blk in range(4):
            v, keys = stgs[blk % 2]
            yv = v.bitcast(F32).rearrange("p a b -> p (a b)")
            for k4 in range(4):
                b = bank()
                for k in range(4):
                    kc = k4 * 4 + k
                    P.op("pe", (lambda b, k, kc, blk: lambda e: e.transpose(ps[:, b, k * 128:(k + 1) * 128], tl.xT[:, kc, blk * 128:(blk + 1) * 128], identf[:]))(b, k, kc, blk),
                         [("x", kc), "identf"], [pk(b)])
                cp("act", yv[:, k4 * 512:(k4 + 1) * 512], ps[:, b, :], [pk(b)], keys)
            r0 = tile * T + blk * 128
            dma("act", y_out[r0:r0 + 128, :], yv, keys, [], "yo%d" % (blk % 2))

    def win_step(l, tile):
        g = "i%d" % l
        c0 = tile * T
        tl.hreset()
        hc = tl.hcarve
        tabs = []
        for i, (tab, idx) in enumerate([(ropeP, 0), (ropeP, 1), (ropeD, 0), (ropeD, 1)]):
            v, keys = hc(2 * T, None)
            tv = v.bitcast(F32).rearrange("p a b -> p (a b)")
            dma("sp", tv, tab[idx, :, c0:c0 + T], [], keys, "tb%d" % i)
            tabs.append((tv, keys))
        (cosP, kcP), (sinP, ksP), (cosD, kcD), (sinD, ksD) = tabs
        v, klow = hc(2 * T, None)
        lowT = v.bitcast(F32).rearrange("p a b -> p (a b)")
        cqv, kcq = hc(3 * T, None)
        ckv, kck = hc(3 * T, None)
        v, kktm = hc(4 * 384, None)
        ktm = v.rearrange("p a b -> p (a b)")[:, 0:1536].rearrange("p (a b) -> p a b", b=384)
        v, ksp = hc(2 * 4 * 384, None)
        sp_tm = v.rearrange("p a b -> p (a b)")[:, 0:3072].rearrange("p (d a b) -> p d a b", d=2, b=384)
        v, ke3 = hc(2 * 2 * 4 * 384, None)
        e3 = v.bitcast(F32).rearrange("p a b -> p (a b)")[:, 0:3072].rearrange("p (d a b) -> p d a b", d=2, b=384)
        tms = []
        for i in range(2):
            v, keys = hc(4 * 256, None)
            tms.append((v.rearrange("p a b -> p (a b)").rearrange("p (a b) -> p a b", b=256), keys))
        v, ket = hc(2 * 384, None)
        etmp = v.bitcast(F32).rearrange("p a b -> p (a b)")[:, 0:384]
        tmr = Ring("tm", 2)

        rmsnorm(l * 52 + 16)
        s = load_A(s_fm[l][0][:, :, 0:32], 32, g)
        b = bank()
        for kc in range(16):
            mm(ps[0:32, b, :], tl.rA[s][:, kc, 0:32], tl.hT[:, kc, :], kc == 0, kc == 15, [("A", s), "hT"], [pk(b)])
        cp("act", lowT[0:32, :], ps[0:32, b, :], [pk(b)], klow)
        P.op("dve", lambda e: e.memset(lowT[32:33, :], 1.0), [], klow)
        for d in range(2):
            for blk in range(4):
                b = bank()
                mm(ps[:, b, 0:384], lowT[0:33, blk * 128:(blk + 1) * 128], upa[0:33, l, d, :], True, True, klow + ["upa"], [pk(b)])
                act(etmp, ps[:, b, 0:384], AF.Exp, [pk(b)], ket, scale=-1.0)
                act(sp_tm[:, d, blk, :], etmp, AF.Ln, ket, ksp, bias=1.0)
        for d in range(2):
            for blk in range(4):
                b = bank()
                mm(ps[:, b, 0:384], cm[:, C_SL if d == 0 else C_SU, :], sp_tm[:, d, blk, :], True, True, ksp + ["cm"], [pk(b)])
                act(e3[:, d, blk, :], ps[:, b, 0:384], AF.Exp, [pk(b)], ke3, scale=-1.0 / 16.0)

        def rope(src_bf, skeys, cosT, sinT, tkeys, rmat, dst):
            b2 = bank()
            mm(ps[:, b2, :], cm[:, rmat, :], src_bf, True, True, skeys + ["cm"], [pk(b2)])
            ta, tb_ = tmpr.next(), tmpr.next()
            tt("dve", tl.tmp[ta], src_bf, cosT, ALU.mult, skeys + tkeys, [("tmp", ta)])
            tt("dve", tl.tmp[tb_], ps[:, b2, :], sinT, ALU.mult, [pk(b2)] + tkeys, [("tmp", tb_)])
            q = stgr.next()
            tt("dve", tl.stg[q], tl.tmp[ta], tl.tmp[tb_], ALU.add, [("tmp", ta), ("tmp", tb_)], [("stg", q)])
            store_stg(q, dst)

        def fm_sub(sub, b):
            pb_ = ps[:, b, :]
            if sub < 3:
                cp("act", cqv[:, sub, :], pb_, [pk(b)], kcq)
            elif sub < 6:
                cp("act", ckv[:, sub - 3, :], pb_, [pk(b)], kck)
                if sub == 5:
                    gla_prep()
            elif sub < 12:
                q = stgr.next()
                act(tl.stg[q], pb_, AF.Silu, [pk(b)], [("stg", q)])
                store_stg(q, sgs[:, sub - 6, c0:c0 + T])
            elif sub < 36:
                dst = [aq, ak, bq, bk][(sub - 12) // 6][:, (sub - 12) % 6, c0:c0 + T]
                u = ubr.next()
                cp("act", tl.ub[u], pb_, [pk(b)], [("ub", u)])
                rope(tl.ub[u], [("ub", u)], cosP, sinP, kcP + ksP, C_RP, dst)
            else:
                isq = sub < 42
                dst = dq[:, sub - 36, c0:c0 + T] if isq else dk[:, sub - 42, c0:c0 + T]
                gcol = l * 52 + 48 + (2 if isq else 3)
                s2 = sqr.next()
                act(tl.sq[s2], pb_, AF.Square, [pk(b)], [("sq", s2)])
                bn = bank()
                mm(ps[:, bn, :], cm[:, C_BLK, :], tl.sq[s2], True, True, [("sq", s2), "cm"], [pk(bn)])
                ta = tmpr.next()
                act(tl.tmp[ta], ps[:, bn, :], AF.Sqrt, [pk(bn), "epsb"], [("tmp", ta)], scale=1.0 / 64.0, bias=epsb[:, 0:1])
                recip(tl.tmp[ta], tl.tmp[ta], [("tmp", ta)], [("tmp", ta)])
                u = ubr.next()
                stt(tl.ub[u], pb_, smp[:, gcol:gcol + 1], tl.tmp[ta], ALU.mult, ALU.mult, [pk(b), ("tmp", ta), "smp"], [("ub", u)])
                rope(tl.ub[u], [("ub", u)], cosD, sinD, kcD + ksD, C_RA, dst)

        def gla_prep():
            for d in range(2):
                for i in range(3):
                    b = bank()
                    for blk in range(4):
                        mm(ps[:, b, blk * 128:(blk + 1) * 128], sp_tm[:, d, blk, i * 128:(i + 1) * 128],
                           cm[:, C_TRIU if d == 0 else C_TRIL, :], True, True, ksp + ["cm"], [pk(b)])
                    ta = tmpr.next()
                    act(tl.tmp[ta], ps[:, b, :], AF.Exp, [pk(b)], [("tmp", ta)], scale=-1.0 / 16.0)
                    col = 127 if d == 0 else 0
                    cp("dve", dec[:, d, i, tile * 4:tile * 4 + 4], tl.tmp[ta][:, col::128], [("tmp", ta)], ["dec"])
                    q = stgr.next()
                    stt(tl.stg[q], cqv[:, i, :], 0.125, tl.tmp[ta], ALU.mult, ALU.mult, kcq + [("tmp", ta)], [("stg", q)])
                    store_stg(q, cqs[d][:, i, c0:c0 + T])
                    tb_ = tmpr.next()
                    act(tl.tmp[tb_], ps[:, b, :], AF.Exp, [pk(b)], [("tmp", tb_)], scale=1.0 / 16.0)
                    q = stgr.next()
                    tt("dve", tl.stg[q], ckv[:, i, :], tl.tmp[tb_], ALU.mult, kck + [("tmp", tb_)], [("stg", q)])
                    store_stg(q, cks[d][:, i, c0:c0 + T])

        for u in range(1, 23):
            s = load_A(s_fm[l][u], 256, g)
            for h in range(2):
                b = bank()
                for kc in range(16):
                    mm(ps[:, b, :], tl.rA[s][:, kc, h * 128:(h + 1) * 128], tl.hT[:, kc, :], kc == 0, kc == 15, [("A", s), "hT"], [pk(b)])
                fm_sub(2 * (u - 1) + h, b)
        tmdst = [av[i] for i in range(6)] + [bv[i] for i in range(6)] + [cv[i] for i in range(6)] + [dvs[i] for i in range(2)]
        for u in range(12):
            ncol = 256 if u < 11 else 128
            s = load_A(s_tm[l][u][:, :, 0:ncol], ncol, g)
            q = tmr.next()
            tmv, tmk = tms[q]
            q2 = None
            for blk in range(4):
                b = bank()
                for kc in range(16):
                    mm(ps[:, b, 0:ncol], tl.hT[:, kc, blk * 128:(blk + 1) * 128], tl.rA[s][:, kc, 0:ncol], kc == 0, kc == 15, [("A", s), "hT"], [pk(b)])
                if u < 10:
                    cp("act", tmv[:, blk, 0:ncol], ps[:, b, 0:ncol], [pk(b)], tmk)
                else:
                    off = (u - 10) * 256
                    cp("act", ktm[:, blk, off:off + ncol], ps[:, b, 0:ncol], [pk(b)], kktm)
            if u < 10:
                for h in range(2):
                    dst = tmdst[2 * u + h]
                    dma("act", dst[:, tile * 4:tile * 4 + 4, :], tmv[:, :, h * 128:(h + 1) * 128], tmk, [], "tm%d%d" % (q, h))
        for d in range(2):
            for pr in range(3):
                q = tmr.next()
                tmv, tmk = tms[q]
                for blk in range(4):
                    tt("dve", tmv[:, blk, 0:128], ktm[:, blk, pr * 128:(pr + 1) * 128], e3[:, d, blk, pr * 128:(pr + 1) * 128], ALU.mult,
                       kktm + ke3, tmk)
                dma("act", khs[d][pr][:, tile * 4:tile * 4 + 4, :], tmv[:, :, 0:128], tmk, [], "tm%d0" % q)

    def store_x(tile):
        for hh in range(2):
            dma("act", xs[:, 8 * hh:8 * hh + 8, tile * T:(tile + 1) * T], tl.xT[:, 8 * hh:8 * hh + 8, :], XK[8 * hh:8 * hh + 8], [], "xst%d" % hh)

    def load_x(tile):
        for hh in range(2):
            dma("sp", tl.xT[:, 8 * hh:8 * hh + 8, :], xs[:, 8 * hh:8 * hh + 8, tile * T:(tile + 1) * T], [], XK[8 * hh:8 * hh + 8], "xld%d" % hh)

    def wout_step(l, tile):
        g = "o%d" % l
        tl.hreset()
        v, keys = tl.hcarve(24 * T, None)
        for hh in range(3):
            dma("sp", v[:, 8 * hh:8 * hh + 8, :], oT[:, 8 * hh:8 * hh + 8, tile * T:(tile + 1) * T], [], keys[8 * hh:8 * hh + 8], "oTl%d" % hh)
        for m in range(16):
            s = load_B(s_o[l][m], 24, g)
            b = bank()
            for kc in range(24):
                mm(ps[:, b, :], tl.rB[s][:, kc, :], v[:, kc, :], kc == 0, kc == 23, [("B", s), ("hid", kc)], [pk(b)])
            tt("dve", tl.xT[:, m, :], ps[:, b, :], tl.xT[:, m, :], ALU.add, [pk(b), ("x", m)], [("x", m)])

    def attn_phase(l):
        c = Carver()
        qTb = [c.bf(NTOK) for _ in range(2)]
        kTb = [c.bf(NTOK) for _ in range(2)]
        Vb = [c.bf(NK, 128) for _ in range(2)]
        Pt = [c.bf(2, T) for _ in range(4)]
        bm = c.bf(20, T)
        tf = [c.f32(T) for _ in range(6)]
        stg = [c.bf(T) for _ in range(4)]
        sqb = c.bf(T)
        for i4 in range(4):
            dma("pool", bm[:, 5 * i4:5 * i4 + 5, :], bmask[:, 5 * i4:5 * i4 + 5, :], [], ["bm"], "c_bm")
        qr, kr, vr, pr_, sr, tfr = Ring("q", 2), Ring("k", 2), Ring("v", 2), Ring("P", 4), Ring("s", 4), Ring("tf", 6)
        sc = [0]
        accp = [0]

        def pair(mode, qsrc, ksrc, vsrc, oc):
            qs, ks_, vs = qr.next(), kr.next(), vr.next()
            hN, hK = NTOK // 2, NK // 2
            lkey = ("ld", qs)
            for hh in range(2):
                dma("sp", qTb[qs][:, hh * hN:(hh + 1) * hN], qsrc[:, hh * hN:(hh + 1) * hN], [], [lkey], "al%d" % qs)
                dma("sp", kTb[ks_][:, hh * hN:(hh + 1) * hN], ksrc[:, hh * hN:(hh + 1) * hN], [], [lkey], "al%d" % qs)
                dma("sp", Vb[vs][:, hh * hK:(hh + 1) * hK, :], vsrc[:, hh * hK:(hh + 1) * hK, :], [], [lkey], "al%d" % qs)
            qT, kT, V = qTb[qs], kTb[ks_], Vb[vs]
            rd = [lkey]
            for qt in range(NT):
                if mode == "B":
                    kts = list(range(max(0, 4 * qt - 8), min(NK, 4 * qt + 12)))
                else:
                    kts = list(range(NK))
                if mode == "A":
                    po, po2, pl, pl2 = 4, 5, 6, 7
                else:
                    po = 4 + 2 * (accp[0] % 2)
                    pl = po + 1
                    accp[0] += 1
                for ki, kt in enumerate(kts):
                    sl = sc[0] % 2
                    sc[0] += 1
                    b0, b1 = 2 * sl, 2 * sl + 1
                    mm(ps[:, b0, :], kT[0:64, kt * 128:(kt + 1) * 128], qT[0:64, qt * T:(qt + 1) * T], True, True, rd, [pk(b0)])
                    mm(ps[:, b1, :], kT[64:128, kt * 128:(kt + 1) * 128], qT[64:128, qt * T:(qt + 1) * T], True, True, rd, [pk(b1)])
                    p = pr_.next()
                    ci = kt * NT + qt
                    act(Pt[p], ps[:, b0:b0 + 2, :], AF.Exp, [pk(b0), pk(b1), "segb"], [("Pt", p)], scale=0.125, bias=segb[:, ci:ci + 1])
                    if mode == "B":
                        r = kt - 4 * qt + 8
                        tt("dve", Pt[p][:, 0, :], Pt[p][:, 0, :], bm[:, r, :], ALU.mult, [("Pt", p), "bm"], [("Pt", p)])
                        tt("dve", Pt[p][:, 1, :], Pt[p][:, 1, :], bm[:, r, :], ALU.mult, [("Pt", p), "bm"], [("Pt", p)])
                    first, last = ki == 0, ki == len(kts) - 1
                    rp = [("Pt", p), lkey, "cm"]
                    if mode == "A":
                        mm(ps[:, po, :], V[:, kt, :], Pt[p][:, 0, :], first, last, rp, [pk(po)])
                        mm(ps[:, po2, :], V[:, kt, :], Pt[p][:, 1, :], first, last, rp, [pk(po2)])
                        mm(ps[:, pl, :], cm[:, C_ONES, :], Pt[p][:, 0, :], first, last, rp, [pk(pl)])
                        mm(ps[:, pl2, :], cm[:, C_ONES, :], Pt[p][:, 1, :], first, last, rp, [pk(pl2)])
                    else:
                        mm(ps[0:64, po, :], V[:, kt, 0:64], Pt[p][:, 0, :], first, last, rp, [pk(po)])
                        mm(ps[64:128, po, :], V[:, kt, 64:128], Pt[p][:, 1, :], first, last, rp, [pk(po)])
                        mm(ps[0:64, pl, :], cm[:, C_ONES, 0:64], Pt[p][:, 0, :], first, last, rp, [pk(pl)])
                        mm(ps[64:128, pl, :], cm[:, C_ONES, 0:64], Pt[p][:, 1, :], first, last, rp, [pk(pl)])
                if mode != "A":
                    a = tfr.next()
                    recip(tf[a], ps[:, pl, :], [pk(pl)], [("tf", a)])
                    q = sr.next()
                    tt("dve", stg[q], ps[:, po, :], tf[a], ALU.mult, [pk(po), ("tf", a)], [("sg", q)])
                    dma("act", oT[:, oc, qt * T:(qt + 1) * T], stg[q], [("sg", q)], [], "as%d" % q)
                else:
                    a1, a2, a3 = tfr.next(), tfr.next(), tfr.next()
                    recip(tf[a1], ps[:, pl, :], [pk(pl)], [("tf", a1)])
                    tt("dve", tf[a1], ps[:, po, :], tf[a1], ALU.mult, [pk(po), ("tf", a1)], [("tf", a1)])
                    recip(tf[a2], ps[:, pl2, :], [pk(pl2)], [("tf", a2)])
                    tt("dve", tf[a2], ps[:, po2, :], tf[a2], ALU.mult, [pk(po2), ("tf", a2)], [("tf", a2)])
                    stt(tf[a3], tf[a2], neglam[:, l:l + 1], tf[a1], ALU.mult, ALU.add, [("tf", a1), ("tf", a2), "neglam"], [("tf", a3)])
                    tt("dve", sqb, tf[a3], tf[a3], ALU.mult, [("tf", a3)], ["sqb"])
                    sl = sc[0] % 2
                    sc[0] += 1
                    bn = 2 * sl
                    mm(ps[:, bn, :], cm[:, C_ONES, :], sqb, True, True, ["sqb", "cm"], [pk(bn)])
                    cc = 1.0 - LAM_INIT[l]
                    act(tf[a1], ps[:, bn, :], AF.Sqrt, [pk(bn), "epsb"], [("tf", a1)], scale=1.0 / (128.0 * cc * cc), bias=epsb[:, 1 + l:2 + l])
                    recip(tf[a1], tf[a1], [("tf", a1)], [("tf", a1)])
                    q = sr.next()
                    gcol = l * 52 + 48
                    stt(stg[q], tf[a3], smp[:, gcol:gcol + 1], tf[a1], ALU.mult, ALU.mult, [("tf", a3), ("tf", a1), "smp"], [("sg", q)])
                    dma("act", oT[:, oc, qt * T:(qt + 1) * T], stg[q], [("sg", q)], [], "as%d" % q)

        import os
        modes = os.environ.get("ATTN_MODES", "DAB")
        if "D" in modes:
            for pr in range(6):
                pair("D", dq[:, pr, :], dk[:, pr // 3, :], dvs[pr // 3], 18 + pr)
        if "A" in modes:
            for h in range(6):
                pair("A", aq[:, h, :], ak[:, h, :], av[h], h)
        if "B" in modes:
            for pr in range(6):
                pair("B", bq[:, pr, :], bk[:, pr, :], bv[pr], 6 + pr)

    def gla_phase(l):
        c = Carver()
        qTb = [c.bf(3, T) for _ in range(2)]
        qzb = [c.bf(2, 3, T) for _ in range(2)]
        kTb = [c.bf(3, T) for _ in range(2)]
        khb = [c.bf(3, 4, 128) for _ in range(2)]
        vb = [c.bf(6, 4, 128) for _ in range(2)]
        ofb = [c.f32(6, T) for _ in range(2)]
        sgb = [c.bf(6, T) for _ in range(2)]
        ofst = [c.f32(6, T) for _ in range(2)]
        ob = c.f32(6, T)
        S = c.f32(3, 128)
        tU = c.f32(3, 128)
        Sb = c.bf(3, 128)
        At = [c.bf(128) for _ in range(6)]
        tf = [c.f32(T) for _ in range(3)]
        sqb = c.bf(T)
        stg = [c.bf(T) for _ in range(2)]
        lr, tfr, sr, ofr = Ring("l", 2), Ring("t", 3), Ring("s", 2), Ring("of", 2)
        gcount = [0]
        for s_ in range(2):
            P.op("dve", (lambda s_: lambda e: e.memset(qzb[s_][:, :, :, :], 0.0))(s_), [], [("gl", s_)])
        decf = c.f32(2, 3, NK)
        for d_ in range(2):
            for pr_ in range(3):
                tt("dve", decf[:, d_, pr_, :], dec[:, d_, pr_, :], gfl[:, d_, :], ALU.mult, ["dec", "gfl"], ["decf"])
        import os
        for d in [int(ch) for ch in os.environ.get("GLA_DIRS", "01")]:
            P.op("dve", lambda e: e.memset(S[:, :, :], 0.0), [], [("S", i) for i in range(3)])
            P.op("dve", lambda e: e.memset(Sb[:, :, :], 0.0), [], [("Sb", i) for i in range(3)])
            tiles = range(NT) if d == 0 else range(NT - 1, -1, -1)
            for tile in tiles:
                c0 = tile * T
                s = lr.next()
                lk = ("gl", s)
                gk_ = "gl%d" % s
                dma("sp", qTb[s], cqs[d][:, :, c0:c0 + T], [], [lk], gk_)
                dma("sp", qzb[s][0:64, 0, :, :], cqs[d][0:64, :, c0:c0 + T], [], [lk], gk_)
                dma("sp", qzb[s][64:128, 1, :, :], cqs[d][64:128, :, c0:c0 + T], [], [lk], gk_)
                dma("sp", kTb[s], cks[d][:, :, c0:c0 + T], [], [lk], gk_)
                for pr in range(3):
                    dma("sp", khb[s][:, pr, :, :], khs[d][pr][:, tile * 4:tile * 4 + 4, :], [], [lk], gk_)
                for h in range(6):
                    dma("sp", vb[s][:, h, :, :], cv[h][:, tile * 4:tile * 4 + 4, :], [], [lk], gk_)
                if d == 1:
                    dma("sp", ofb[s], ofs[:, :, c0:c0 + T], [("ofs", tile)], [lk], gk_)
                    dma("sp", sgb[s], sgs[:, :, c0:c0 + T], [], [lk], gk_)
                else:
                    fo = ofr.next()
                qz = qzb[s]
                qT, kT, kh, v = qTb[s], kTb[s], khb[s], vb[s]
                chunks = range(4) if d == 0 else range(3, -1, -1)
                for cch in chunks:
                    kt = tile * 4 + cch
                    cs = slice(cch * 128, (cch + 1) * 128)
                    par = gcount[0] % 2
                    gcount[0] += 1
                    bA = (0, 1) if par == 0 else (5, 6)
                    bU = 2 if par == 0 else 7
                    bO = (3, 4)
                    for h in range(6):
                        pr, hp = h // 2, h % 2
                        rows = slice(64 * hp, 64 * hp + 64)
                        oa = (h // 2) * 128
                        mm(ps[:, bA[hp], oa:oa + 128], kT[rows, pr, cs], qT[rows, pr, cs], True, True, [lk], [pk(bA[hp])])
                        mm(ps[rows, bU, pr * 128:(pr + 1) * 128], kh[:, pr, cch, hp * 64:hp * 64 + 64], v[:, h, cch, :], True, True, [lk], [pk(bU)])
                    for h in range(6):
                        hp = h % 2
                        oa = (h // 2) * 128
                        tt("dve", At[h], ps[:, bA[hp], oa:oa + 128], cm[:, C_TRIU if d == 0 else C_TRIL, :], ALU.mult, [pk(bA[hp]), "cm"], [("At", h)])
                    for h in range(6):
                        pr, hp = h // 2, h % 2
                        oo = (h // 2) * 128
                        mm(ps[:, bO[hp], oo:oo + 128], v[:, h, cch, :], At[h], True, False, [lk, ("At", h)], [pk(bO[hp])])
                        mm(ps[:, bO[hp], oo:oo + 128], Sb[:, pr, :], qz[:, hp, pr, cs], False, True, [lk, ("Sb", pr)], [pk(bO[hp])])
                    for pr in range(3):
                        P.op("act", (lambda pr, d, kt, bU: lambda e: e.activation(out=tU[:, pr, :], in_=ps[:, bU, pr * 128:(pr + 1) * 128], func=AF.Copy, scale=gfl[:, d, kt:kt + 1]))(pr, d, kt, bU),
                             [pk(bU), "gfl"], [("tU", pr)])
                        stt(S[:, pr, :], S[:, pr, :], decf[:, d, pr, kt:kt + 1], tU[:, pr, :], ALU.mult, ALU.add,
                            [("S", pr), ("tU", pr), "decf"], [("S", pr)])
                        cp("act", Sb[:, pr, :], S[:, pr, :], [("S", pr)], [("Sb", pr)])
                    for h in range(6):
                        hp = h % 2
                        oo = (h // 2) * 128
                        if d == 0:
                            cp("act", ofst[fo][:, h, cs], ps[:, bO[hp], oo:oo + 128], [pk(bO[hp])], [("ofst", fo)])
                        else:
                            tt("dve", ob[:, h, cs], ps[:, bO[hp], oo:oo + 128], ofb[s][:, h, cs], ALU.add, [pk(bO[hp]), lk], [("ob", h)])
                if d == 0:
                    dma("act", ofs[:, :, c0:c0 + T], ofst[fo], [("ofst", fo)], [("ofs", tile)], "gf%d" % fo)
                else:
                    gcol = l * 52 + 49
                    for h in range(6):
                        tt("dve", sqb, ob[:, h, :], ob[:, h, :], ALU.mult, [("ob", h)], ["sqb"])
                        mm(ps[:, 5, :], cm[:, C_ONES, :], sqb, True, True, ["sqb", "cm"], [pk(5)])
                        a = tfr.next()
                        act(tf[a], ps[:, 5, :], AF.Sqrt, [pk(5), "epsb"], [("tf", a)], scale=1.0 / 128.0, bias=epsb[:, 0:1])
                        recip(tf[a], tf[a], [("tf", a)], [("tf", a)])
                        stt(tf[a], ob[:, h, :], smp[:, gcol:gcol + 1], tf[a], ALU.mult, ALU.mult, [("ob", h), ("tf", a), "smp"], [("tf", a)])
                        q = sr.next()
                        tt("dve", stg[q], tf[a], sgb[s][:, h, :], ALU.mult, [("tf", a), lk], [("sg", q)])
                        dma("act", oT[:, 12 + h, c0:c0 + T], stg[q], [("sg", q)], [], "gs_%d" % q)

    if "p1" in stages:
        for tile in range(NT):
            load_x_input(tile)
            ffn(0, 0, 0)
            win_step(0, tile)
            store_x(tile)
    for l in range(nlayers):
        if "attn" in stages:
            P.barrier()
            attn_phase(l)
        if "gla" in stages:
            P.barrier()
            gla_phase(l)
        if "p3" in stages:
            P.barrier()
            for tile in range(NT):
                load_x(tile)
                wout_step(l, tile)
                ffn(l, 1, l * 52 + 32)
                if l + 1 < DEPTH:
                    ffn(l + 1, 0, (l + 1) * 52)
                    win_step(l + 1, tile)
                    store_x(tile)
                else:
                    final_out(tile)
    P.run()
    return nc


def _const_mats():
    m = np.zeros((128, 9, 128), np.float32)
    i = np.arange(128)
    m[:, C_ID, :] = np.eye(128, dtype=np.float32)
    m[:, C_ONES, :] = 1.0
    m[:, C_BLK, :] = (i[:, None] // 64 == i[None, :] // 64).astype(np.float32)
    rp = np.zeros((128, 128), np.float32)
    ra = np.zeros((128, 128), np.float32)
    for g in range(2):
        o = 64 * g
        for j in range(8):
            rp[o + j + 8, o + j] = -1.0
            rp[o + j, o + j + 8] = 1.0
        for base in (0, 32):
            for j in range(16):
                ra[o + base + j + 16, o + base + j] = -1.0
                ra[o + base + j, o + base + j + 16] = 1.0
    m[:, C_RP, :] = rp
    m[:, C_RA, :] = ra
    s, t = i[:, None], i[None, :]
    m[:, C_TRIU, :] = (s <= t)
    m[:, C_TRIL, :] = (s >= t)
    m[:, C_SL, :] = (s > t)
    m[:, C_SU, :] = (s < t)
    return m


def _tables(NTOK, seglen):
    NT, NK = NTOK // T, NTOK // 128
    pos = (np.arange(NTOK) % seglen).astype(np.float32)
    r = np.arange(128) % 64
    invP = (500000.0 ** (-(np.arange(0, 16, 2, dtype=np.float32)) / 16.0)).astype(np.float32)
    ropeP = np.zeros((2, 128, NTOK), np.float32)
    ropeP[0] = 1.0
    for row in range(128):
        if r[row] < 16:
            ang = pos * invP[r[row] % 8]
            ropeP[0, row] = np.cos(ang)
            ropeP[1, row] = np.sin(ang)
    invD = (10000.0 ** (-(np.arange(0, 32, 2, dtype=np.float32)) / 32.0)).astype(np.float32)
    rowpos = np.floor(pos / 64.0).astype(np.float32)
    colpos = (pos % 64).astype(np.float32)
    ropeD = np.zeros((2, 128, NTOK), np.float32)
    for row in range(128):
        i = r[row]
        p = rowpos if i < 32 else colpos
        ang = p * invD[i % 16]
        ropeD[0, row] = np.cos(ang)
        ropeD[1, row] = np.sin(ang)
    seg_k = (np.arange(NK) * 128) // seglen
    seg_q = (np.arange(NT) * T) // seglen
    sb = np.where(seg_k[:, None] == seg_q[None, :], 0.0, -30000.0).astype(np.float32).reshape(1, NK * NT)
    segbias = np.repeat(sb, 128, 0)
    kk = np.arange(128)[:, None]
    qq = np.arange(512)[None, :]
    bmask = np.zeros((128, 20, 512), np.float32)
    for rr in range(20):
        dd = (rr - 8) * 128 + kk - qq
        ad = np.abs(dd)
        mult = (ad <= 64).astype(np.float32) + ((dd % 4 == 0) & (ad <= 256)) + ((dd % 16 == 0) & (ad <= 1024))
        bmask[:, rr, :] = mult
    gflag = np.ones((128, 2, NK), np.float32)
    cps = seglen // 128
    for kt in range(NK):
        if (kt + 1) % cps == 0:
            gflag[:, 0, kt] = 0.0
        if kt % cps == 0:
            gflag[:, 1, kt] = 0.0
    return ropeP, ropeD, segbias, bmask, gflag


def _layout_weights(inp):
    f = lambda k: np.ascontiguousarray(np.asarray(inp[k], np.float32))
    w_in = f("w_in")
    sp = np.cumsum([0, 768, 768, 768, 768, 768, 768, 384, 384, 768, 768, 32, 768, 256, 256])
    a_q, a_k, a_v, b_q, b_k, b_v, c_q, c_k, c_v, c_g, c_low, d_q, d_k, d_v = [np.arange(sp[i], sp[i + 1]) for i in range(14)]
    dperm = [0, 3, 1, 4, 2, 5, 6, 9, 7, 10, 8, 11]
    d_qp = np.concatenate([d_q[h * 64:(h + 1) * 64] for h in dperm])
    fm_cols = np.concatenate([c_low, c_q, c_k, c_g, a_q, a_k, b_q, b_k, d_qp, d_k])
    tm_cols = np.concatenate([a_v, b_v, c_v, d_v, c_k])
    assert fm_cols.size == NFM and tm_cols.size == NTM
    w_out = f("w_out")
    o_d = 2304 + np.concatenate([np.arange(h * 64, (h + 1) * 64) for h in dperm])
    orow = np.concatenate([np.arange(0, 2304), o_d])
    out = {
        "wg1": f("ffn1_w_gate"), "wu1": f("ffn1_w_up"), "wd1": f("ffn1_w_down"),
        "wg2": f("ffn2_w_gate"), "wu2": f("ffn2_w_up"), "wd2": f("ffn2_w_down"),
        "winfm": np.ascontiguousarray(w_in[:, :, fm_cols]),
        "wintm": np.ascontiguousarray(w_in[:, :, tm_cols]),
        "wout": np.ascontiguousarray(w_out[:, orow, :]),
    }
    NSP = DEPTH * 52 + 16
    smallp = np.zeros((128, NSP), np.float32)
    fm = lambda v: np.asarray(v, np.float32).reshape(16, 128).T
    for l in range(DEPTH):
        o = l * 52
        smallp[:, o:o + 16] = fm(inp["ffn1_norm"][l])
        smallp[:, o + 16:o + 32] = fm(inp["mix_norm"][l])
        smallp[:, o + 32:o + 48] = fm(inp["ffn2_norm"][l])
        smallp[:, o + 48] = np.asarray(inp["diff_out_norm"][l], np.float32)
        smallp[:, o + 49] = np.asarray(inp["gla_out_norm"][l], np.float32)
        smallp[:, o + 50] = np.tile(np.asarray(inp["gqa_q_norm"][l], np.float32), 2)
        smallp[:, o + 51] = np.tile(np.asarray(inp["gqa_k_norm"][l], np.float32), 2)
    smallp[:, DEPTH * 52:DEPTH * 52 + 16] = fm(inp["final_norm"])
    out["smallp"] = smallp
    lamp = np.zeros((64, 4 * DEPTH), np.float32)
    for l in range(DEPTH):
        for j, k in enumerate(["diff_lambda_q1", "diff_lambda_k1", "diff_lambda_q2", "diff_lambda_k2"]):
            lamp[:, 4 * l + j] = np.asarray(inp[k][l], np.float32)
    out["lamp"] = lamp
    upaug = np.zeros((33, DEPTH, 2, 384), np.float32)
    for l in range(DEPTH):
        upaug[0:16, l, 0] = np.asarray(inp["gla_gate_up_f"][l], np.float32)
        upaug[32, l, 0] = np.asarray(inp["gla_gate_bias_f"][l], np.float32)
        upaug[16:32, l, 1] = np.asarray(inp["gla_gate_up_b"][l], np.float32)
        upaug[32, l, 1] = np.asarray(inp["gla_gate_bias_b"][l], np.float32)
    out["upaug"] = upaug
    out["cmat"] = _const_mats()
    return out


def make_in_maps(core_x, seglens, inp):
    NTOK = core_x[0].shape[0]
    shared = _layout_weights(inp)
    tabs = {}
    maps = []
    for x, sl in zip(core_x, seglens):
        if sl not in tabs:
            tabs[sl] = _tables(NTOK, sl)
        ropeP, ropeD, segbias, bmask, gflag = tabs[sl]
        m = dict(shared)
        m.update({"x": np.ascontiguousarray(x, np.float32), "ropeP": ropeP, "ropeD": ropeD, "segbias": segbias, "bmask": bmask, "gflag": gflag})
        maps.append(m)
    return maps


def kernel(**inputs):
    xp = np.asarray(inputs["x_prompt"], np.float32)
    xs_ = np.asarray(inputs["x_sample"], np.float32)
    NTOK = 8192
    core_x = [xp[i] for i in range(4)] + [xs_[4 * i:4 * i + 4].reshape(NTOK, D) for i in range(4)]
    seglens = [8192] * 4 + [2048] * 4
    maps = make_in_maps(core_x, seglens, inputs)
    nc = build(NTOK)
    res = run_bass_kernel_spmd(nc, maps, core_ids=list(range(8)))
    ys = [np.asarray(r["y"], np.float32) for r in res.results]
    y_prompt = np.stack(ys[0:4], 0)
    y_sample = np.concatenate([y.reshape(4, 2048, D) for y in ys[4:8]], 0)
    return (y_prompt, y_sample)
```
